# Optimizing a Trainium2 kernel written in Bass

```python
import math
import jax, jax.numpy as jnp
from jax import lax
import numpy as np

D_MODEL = 1024
BATCH = 8
SEQ = 8192
DEPTH = 2
DEC_BATCH = 16
DEC_SEQ = 16
PAST_LEN = 4096

CHUNK = 64
N_MIXERS = 2
N_S5 = (DEPTH + 1) // 2
N_GDN = DEPTH // 2
EPS = 1e-6

S5_WIDTH = D_MODEL
S5_GROUP = 16
S5_GROUPS = S5_WIDTH // S5_GROUP
S5_STATE = 64

GDN_DK = 128
GDN_DV = 128
GDN_QK_HEADS = D_MODEL // 256
GDN_V_HEADS = 2 * GDN_QK_HEADS
GDN_QK_WIDTH = GDN_QK_HEADS * GDN_DK
GDN_V_WIDTH = GDN_V_HEADS * GDN_DV
GDN_CONV = 4
GDN_CONV_CH = 2 * GDN_QK_WIDTH + GDN_V_WIDTH
GDN_IN = GDN_CONV_CH + GDN_V_WIDTH + 2 * GDN_V_HEADS

kernel_name = "s5_gdn_hybrid_stream_step"

F32 = jnp.float32


def rms_norm(x, g):
    xf = x.astype(F32)
    y = xf * lax.rsqrt(jnp.mean(xf * xf, axis=-1, keepdims=True) + EPS)
    return (y * g.astype(F32)).astype(x.dtype)


def l2norm(x):
    return x * lax.rsqrt(jnp.sum(x * x, axis=-1, keepdims=True) + EPS)


def s5_discretize(log_step, lam_re, lam_im, b_re, b_im):
    step = jnp.exp(log_step.astype(F32))[:, None]
    lr, li = lam_re.astype(F32), lam_im.astype(F32)
    mag = jnp.exp(lr * step)
    ar, ai = mag * jnp.cos(li * step), mag * jnp.sin(li * step)
    den = lr * lr + li * li
    xr = ar - 1.0
    nr = (xr * lr + ai * li) / den
    ni = (ai * lr - xr * li) / den
    br, bi = b_re.astype(F32), b_im.astype(F32)
    bbr = nr[..., None] * br - ni[..., None] * bi
    bbi = nr[..., None] * bi + ni[..., None] * br
    return ar, ai, bbr, bbi


def _cplx_combine(e1, e2):
    a1r, a1i, b1r, b1i = e1
    a2r, a2i, b2r, b2i = e2
    return (a2r * a1r - a2i * a1i, a2r * a1i + a2i * a1r,
            a2r * b1r - a2i * b1i + b2r, a2r * b1i + a2i * b1r + b2i)


def s5_scan(u, h_re, h_im, ar, ai, bbr, bbi, c_re, c_im):
    bsz, L, E = u.shape
    T = CHUNK if L % CHUNK == 0 else L
    n = L // T
    ug = u.reshape(bsz, n, T, S5_GROUPS, S5_GROUP).transpose(1, 0, 2, 3, 4)
    a_r = jnp.broadcast_to(ar[None, None], (bsz, T, S5_GROUPS, S5_STATE))
    a_i = jnp.broadcast_to(ai[None, None], (bsz, T, S5_GROUPS, S5_STATE))

    def step(carry, uc):
        hr, hi = carry
        br = jnp.einsum('gpc,btgc->btgp', bbr, uc)
        bi = jnp.einsum('gpc,btgc->btgp', bbi, uc)
        pr, pi, sr, si = lax.associative_scan(_cplx_combine, (a_r, a_i, br, bi), axis=1)
        xr = pr * hr[:, None] - pi * hi[:, None] + sr
        xi = pr * hi[:, None] + pi * hr[:, None] + si
        y = jnp.einsum('gcp,btgp->btgc', c_re, xr) - jnp.einsum('gcp,btgp->btgc', c_im, xi)
        return (xr[:, -1], xi[:, -1]), y

    (hr, hi), ys = lax.scan(step, (h_re, h_im), ug)
    y = ys.transpose(1, 0, 2, 3, 4).reshape(bsz, L, E)
    return y, hr, hi


def s5_branch(h, h_re, h_im, w_in, log_step, lam_re, lam_im, b_re, b_im,
              c_re, c_im, d, w_glu, b_glu, w_out):
    proj = h @ w_in
    u, z = jnp.split(proj, 2, axis=-1)
    uf = u.astype(F32)
    ar, ai, bbr, bbi = s5_discretize(log_step, lam_re, lam_im, b_re, b_im)
    y, hr, hi = s5_scan(uf, h_re.astype(F32), h_im.astype(F32), ar, ai, bbr, bbi,
                        c_re.astype(F32), c_im.astype(F32))
    y = jax.nn.gelu(y + d.astype(F32) * uf)
    y = y * jax.nn.sigmoid(y @ w_glu.astype(F32) + b_glu.astype(F32))
    y = y * jax.nn.silu(z.astype(F32))
    return y.astype(h.dtype) @ w_out, hr, hi


def gated_delta_rule(q, k, v, beta, g, S0):
    bsz, L, H, _ = q.shape
    T = CHUNK if L % CHUNK == 0 else L
    n = L // T

    def chunks(t):
        t = t.reshape((bsz, n, T, H) + t.shape[3:])
        return jnp.moveaxis(t, (1, 3), (0, 2))

    causal = jnp.tril(jnp.ones((T, T), bool))
    strict = jnp.tril(jnp.ones((T, T), bool), -1)
    eye = jnp.eye(T, dtype=F32)

    def step(S, inp):
        qc, kc, vc, bc, gc = inp
        gcum = jnp.cumsum(gc, axis=-1)
        decay = jnp.exp(jnp.where(causal, gcum[..., :, None] - gcum[..., None, :], -jnp.inf))
        kb = kc * bc[..., None]
        m = jnp.where(strict, jnp.einsum('bhik,bhjk->bhij', kb, kc) * decay, 0.0)
        rhs = jnp.concatenate([vc * bc[..., None], kb * jnp.exp(gcum)[..., None]], axis=-1)
        sol = lax.linalg.triangular_solve(eye + m, rhs, left_side=True, lower=True,
                                          unit_diagonal=True)
        u, w = sol[..., :GDN_DV], sol[..., GDN_DV:]
        v_new = u - jnp.einsum('bhtk,bhkv->bhtv', w, S)
        qk = jnp.einsum('bhik,bhjk->bhij', qc, kc) * decay
        o = (jnp.einsum('bhtk,bhkv->bhtv', qc * jnp.exp(gcum)[..., None], S)
             + jnp.einsum('bhij,bhjv->bhiv', qk, v_new))
        g_last = gcum[..., -1]
        S_new = (S * jnp.exp(g_last)[..., None, None]
                 + jnp.einsum('bhtk,bhtv->bhkv',
                              kc * jnp.exp(g_last[..., None] - gcum)[..., None], v_new))
        return S_new, o

    S, os_ = lax.scan(step, S0, (chunks(q), chunks(k), chunks(v), chunks(beta), chunks(g)))
    o = jnp.moveaxis(os_, (0, 2), (1, 3)).reshape(bsz, L, H, GDN_DV)
    return o, S


def gdn_branch(h, conv_hist, S0, w_in, conv_w, a_log, dt_bias, norm_g, w_out):
    bsz, L, _ = h.shape
    proj = h @ w_in
    o1 = GDN_CONV_CH
    o2 = o1 + GDN_V_WIDTH
    o3 = o2 + GDN_V_HEADS
    qkv, z, b_raw, a_raw = proj[..., :o1], proj[..., o1:o2], proj[..., o2:o3], proj[..., o3:]
    full = jnp.concatenate([conv_hist.astype(qkv.dtype), qkv], axis=1)
    conv = full[:, 0:L] * conv_w[0]
    for j in range(1, GDN_CONV):
        conv = conv + full[:, j:j + L] * conv_w[j]
    new_hist = full[:, L:]
    act = jax.nn.silu(conv.astype(F32))
    q = act[..., :GDN_QK_WIDTH].reshape(bsz, L, GDN_QK_HEADS, GDN_DK)
    k = act[..., GDN_QK_WIDTH:2 * GDN_QK_WIDTH].reshape(bsz, L, GDN_QK_HEADS, GDN_DK)
    v = act[..., 2 * GDN_QK_WIDTH:].reshape(bsz, L, GDN_V_HEADS, GDN_DV)
    rep = GDN_V_HEADS // GDN_QK_HEADS
    q = jnp.repeat(l2norm(q) * (GDN_DK ** -0.5), rep, axis=2)
    k = jnp.repeat(l2norm(k), rep, axis=2)
    beta = jax.nn.sigmoid(b_raw.astype(F32))
    g = -jnp.exp(a_log.astype(F32)) * jax.nn.softplus(a_raw.astype(F32) + dt_bias.astype(F32))
    o, S = gated_delta_rule(q, k, v, beta, g, S0.astype(F32))
    o = rms_norm(o, norm_g) * jax.nn.silu(z.astype(F32).reshape(bsz, L, GDN_V_HEADS, GDN_DV))
    out = o.reshape(bsz, L, GDN_V_WIDTH).astype(h.dtype) @ w_out
    return out, new_hist, S


def trunk(x, c, s5_re0, s5_im0, gdn_s0, gdn_conv0, weights):
    (norm_g, w_ada, b_ada, s5_w_in, s5_log_step, s5_lambda_re, s5_lambda_im, s5_b_re, s5_b_im,
     s5_c_re, s5_c_im, s5_d, s5_w_glu, s5_b_glu, s5_w_out, gdn_w_in, gdn_conv_w, gdn_a_log,
     gdn_dt_bias, gdn_norm_g, gdn_w_out, final_g) = weights
    s5_re_out, s5_im_out, gdn_s_out, gdn_conv_out = [], [], [], []
    cs = jax.nn.silu(c)
    for i in range(DEPTH):
        j = i // N_MIXERS
        mod = cs @ w_ada[i] + b_ada[i]
        shift, scale, gate = jnp.split(mod[:, None, :], 3, axis=-1)
        h = rms_norm(x, norm_g[i]) * (1.0 + scale) + shift
        if i % N_MIXERS == 0:
            out, hr, hi = s5_branch(h, s5_re0[j], s5_im0[j], s5_w_in[j], s5_log_step[j],
                                    s5_lambda_re[j], s5_lambda_im[j], s5_b_re[j], s5_b_im[j],
                                    s5_c_re[j], s5_c_im[j], s5_d[j], s5_w_glu[j], s5_b_glu[j],
                                    s5_w_out[j])
            s5_re_out.append(hr)
            s5_im_out.append(hi)
        else:
            out, hist, S = gdn_branch(h, gdn_conv0[j], gdn_s0[j], gdn_w_in[j], gdn_conv_w[j],
                                      gdn_a_log[j], gdn_dt_bias[j], gdn_norm_g[j], gdn_w_out[j])
            gdn_conv_out.append(hist)
            gdn_s_out.append(S)
        x = x + gate * out
    y = rms_norm(x, final_g)
    return (y, jnp.stack(s5_re_out), jnp.stack(s5_im_out), jnp.stack(gdn_s_out),
            jnp.stack(gdn_conv_out))


def setup_inputs(seed: int = 0) -> dict:
    key = jax.random.key(seed)
    ks = iter(jax.random.split(key, 40))

    def nrm(shape, s):
        return jax.random.normal(next(ks), shape, F32) * s

    def unif(shape, lo, hi):
        return jax.random.uniform(next(ks), shape, F32, minval=lo, maxval=hi)

    D, E, G, P = D_MODEL, S5_WIDTH, S5_GROUPS, S5_STATE
    lam_im = jnp.broadcast_to(jnp.pi * jnp.arange(P, dtype=F32), (N_S5, G, P))
    dt = jnp.exp(unif((N_GDN, GDN_V_HEADS), math.log(1e-3), math.log(1e-1)))
    return {
        "x_prompt": nrm((BATCH, SEQ, D), 1.0),
        "x_sample": nrm((DEC_BATCH, DEC_SEQ, D), 1.0),
        "c_prompt": nrm((BATCH, D), 1.0),
        "c_sample": nrm((DEC_BATCH, D), 1.0),
        "state_s5_re": nrm((N_S5, DEC_BATCH, G, P), 0.1),
        "state_s5_im": nrm((N_S5, DEC_BATCH, G, P), 0.1),
        "state_gdn": nrm((N_GDN, DEC_BATCH, GDN_V_HEADS, GDN_DK, GDN_DV), 0.1),
        "state_gdn_conv": nrm((N_GDN, DEC_BATCH, GDN_CONV - 1, GDN_CONV_CH), 1.0),
        "norm_g": 1.0 + nrm((DEPTH, D), 0.02),
        "w_ada": nrm((DEPTH, D, 3 * D), 0.5 * D ** -0.5),
        "b_ada": nrm((DEPTH, 3 * D), 0.02),
        "s5_w_in": nrm((N_S5, D, 2 * E), D ** -0.5),
        "s5_log_step": unif((N_S5, G), math.log(1e-3), math.log(1e-1)),
        "s5_lambda_re": -0.5 + nrm((N_S5, G, P), 0.01),
        "s5_lambda_im": lam_im + nrm((N_S5, G, P), 0.01),
        "s5_b_re": nrm((N_S5, G, P, S5_GROUP), (2 * S5_GROUP) ** -0.5),
        "s5_b_im": nrm((N_S5, G, P, S5_GROUP), (2 * S5_GROUP) ** -0.5),
        "s5_c_re": nrm((N_S5, G, S5_GROUP, P), 0.5),
        "s5_c_im": nrm((N_S5, G, S5_GROUP, P), 0.5),
        "s5_d": nrm((N_S5, E), 1.0),
        "s5_w_glu": nrm((N_S5, E, E), E ** -0.5),
        "s5_b_glu": nrm((N_S5, E), 0.02),
        "s5_w_out": nrm((N_S5, E, D), E ** -0.5),
        "gdn_w_in": nrm((N_GDN, D, GDN_IN), D ** -0.5),
        "gdn_conv_w": nrm((N_GDN, GDN_CONV, GDN_CONV_CH), 0.5),
        "gdn_a_log": jnp.log(unif((N_GDN, GDN_V_HEADS), 1.0, 16.0)),
        "gdn_dt_bias": dt + jnp.log(-jnp.expm1(-dt)),
        "gdn_norm_g": 1.0 + nrm((N_GDN, GDN_DV), 0.02),
        "gdn_w_out": nrm((N_GDN, GDN_V_WIDTH, D), GDN_V_WIDTH ** -0.5),
        "final_g": 1.0 + nrm((D,), 0.02),
    }


def reference(x_prompt, x_sample, c_prompt, c_sample, state_s5_re, state_s5_im, state_gdn,
              state_gdn_conv, norm_g, w_ada, b_ada, s5_w_in, s5_log_step, s5_lambda_re,
              s5_lambda_im, s5_b_re, s5_b_im, s5_c_re, s5_c_im, s5_d, s5_w_glu, s5_b_glu,
              s5_w_out, gdn_w_in, gdn_conv_w, gdn_a_log, gdn_dt_bias, gdn_norm_g, gdn_w_out,
              final_g):
    weights = (norm_g, w_ada, b_ada, s5_w_in, s5_log_step, s5_lambda_re, s5_lambda_im, s5_b_re,
               s5_b_im, s5_c_re, s5_c_im, s5_d, s5_w_glu, s5_b_glu, s5_w_out, gdn_w_in,
               gdn_conv_w, gdn_a_log, gdn_dt_bias, gdn_norm_g, gdn_w_out, final_g)
    bp = x_prompt.shape[0]
    z_s5 = jnp.zeros((N_S5, bp, S5_GROUPS, S5_STATE), F32)
    z_gdn = jnp.zeros((N_GDN, bp, GDN_V_HEADS, GDN_DK, GDN_DV), F32)
    z_conv = jnp.zeros((N_GDN, bp, GDN_CONV - 1, GDN_CONV_CH), x_prompt.dtype)
    y_prompt, s5r_p, s5i_p, gdn_p, conv_p = trunk(x_prompt, c_prompt, z_s5, z_s5, z_gdn,
                                                  z_conv, weights)
    y_sample, s5r_s, s5i_s, gdn_s, conv_s = trunk(x_sample, c_sample, state_s5_re, state_s5_im,
                                                  state_gdn, state_gdn_conv, weights)
    return (y_prompt, y_sample, s5r_p, s5i_p, gdn_p, conv_p, s5r_s, s5i_s, gdn_s, conv_s)
```

```python
import contextlib
import numpy as np
import concourse.bass as bass
import concourse.mybir as mybir
from concourse.bass_utils import run_bass_kernel_spmd

F32 = mybir.dt.float32
BF16 = mybir.dt.bfloat16
AF = mybir.ActivationFunctionType
ALU = mybir.AluOpType
D = 1024
EPS = 1e-6
NCORE = 8
LS = 16
GIN = 3088


class Tok:
    __slots__ = ("sem", "val")

    def __init__(self, sem, val):
        self.sem = sem
        self.val = val


class Eng:
    def __init__(self, kb, eng, name, is_pe=False):
        self.eng = eng
        self.sem = kb.es.enter_context(kb.nc.semaphore(name))
        self.cnt = 0
        self.seen = {}
        self.is_pe = is_pe

    def wait(self, toks):
        for t in toks:
            if t is None:
                continue
            if self.is_pe and t.sem is self.sem:
                continue
            key = id(t.sem)
            if self.seen.get(key, 0) >= t.val:
                continue
            self.seen[key] = t.val
            self.eng.wait_ge(t.sem, t.val)

    def done(self, ins):
        self.cnt += 1
        ins.then_inc(self.sem, 1)
        return Tok(self.sem, self.cnt)

    def all_toks(self):
        return [Tok(self.sem, self.cnt)] if self.cnt else []


class DQ:
    def __init__(self, kb, eng, name, nslot=6):
        self.eng = eng
        self.sems = [kb.es.enter_context(kb.nc.semaphore("%s%d" % (name, i))) for i in range(nslot)]
        self.cnts = [0] * nslot
        self.i = 0
        self.seen = {}

    def wait(self, toks):
        for t in toks:
            if t is None:
                continue
            key = id(t.sem)
            if self.seen.get(key, 0) >= t.val:
                continue
            self.seen[key] = t.val
            self.eng.wait_ge(t.sem, t.val)

    def slot(self):
        i = self.i
        self.i = (self.i + 1) % len(self.sems)
        if self.cnts[i]:
            self.wait([Tok(self.sems[i], self.cnts[i])])
        return i

    def done(self, ins, i):
        self.cnts[i] += 16
        ins.then_inc(self.sems[i], 16)
        return Tok(self.sems[i], self.cnts[i])

    def all_toks(self):
        return [Tok(s, c) for s, c in zip(self.sems, self.cnts) if c]


def tname(ap):
    return ap.tensor.name


DRAM_NAMES = set()


def _aps(lst):
    return [a for a in lst if hasattr(a, "tensor") and a.tensor.name not in DRAM_NAMES]


class KB:
    def __init__(self):
        self.nc = bass.Bass("TRN2", target_bir_lowering=False)
        self.es = contextlib.ExitStack()
        nc = self.nc
        self.act = Eng(self, nc.scalar, "sact")
        self.dve = Eng(self, nc.vector, "sdve")
        self.pool = Eng(self, nc.gpsimd, "spool")
        self.pe = Eng(self, nc.tensor, "spe", is_pe=True)
        self.sy = DQ(self, nc.sync, "qsy", 8)
        self.units = [self.act, self.dve, self.pool, self.pe, self.sy]
        self.lastw = {}
        self.reads = {}
        self.rr = 0
        self.uid = 0

    def deps(self, outs, ins):
        outs, ins = _aps(outs), _aps(ins)
        d = []
        for a in ins:
            t = self.lastw.get(tname(a))
            if t is not None:
                d.append(t)
        for a in outs:
            n = tname(a)
            t = self.lastw.get(n)
            if t is not None:
                d.append(t)
            d.extend(self.reads.get(n, {}).values())
        return d

    def record(self, outs, ins, tok):
        outs, ins = _aps(outs), _aps(ins)
        for a in ins:
            self.reads.setdefault(tname(a), {})[id(tok.sem)] = tok
        for a in outs:
            n = tname(a)
            self.lastw[n] = tok
            self.reads[n] = {}

    def op(self, E, fn, outs, ins, *args, **kw):
        E.wait(self.deps(outs, ins))
        tok = E.done(fn(*args, **kw))
        self.record(outs, ins, tok)
        return tok

    def dma(self, out, in_, q=None, **kw):
        q = q or self.sy
        i = q.slot()
        q.wait(self.deps([out], [in_]))
        tok = q.done(q.eng.dma_start(out=out, in_=in_, **kw), i)
        self.record([out], [in_], tok)
        return tok

    def barrier(self):
        toks = []
        for u in self.units:
            toks.extend(u.all_toks())
        for u in self.units:
            u.wait(toks)

    def sb(self, st, shape, dt=F32, name=None):
        self.uid += 1
        return st.enter_context(self.nc.sbuf_tensor("%s_%d" % (name or "t", self.uid), list(shape), dt))

    def dram(self, name, shape, dt=F32, kind="Internal"):
        DRAM_NAMES.add(name)
        return self.nc.dram_tensor(name, list(shape), dt, kind=kind).ap()

    def tt(self, E, out, a, b, op):
        return self.op(E, E.eng.tensor_tensor, [out], [a, b], out=out, in0=a, in1=b, op=op)

    def ts(self, E, out, a, s1, s2, op0, op1=None, sins=()):
        if op1 is None:
            return self.op(E, E.eng.tensor_scalar, [out], [a] + list(sins), out=out, in0=a, scalar1=s1,
                           scalar2=None, op0=op0)
        return self.op(E, E.eng.tensor_scalar, [out], [a] + list(sins), out=out, in0=a, scalar1=s1,
                       scalar2=s2, op0=op0, op1=op1)

    def stt(self, out, a, s, b, op0, op1):
        return self.op(self.dve, self.nc.vector.scalar_tensor_tensor, [out], [a, s, b], out=out, in0=a,
                       scalar=s, in1=b, op0=op0, op1=op1)

    def actf(self, out, a, func, bias=None, scale=None, accum=None):
        kw = {}
        ins = [a]
        outs = [out]
        if bias is not None:
            kw["bias"] = bias
            if not isinstance(bias, (int, float)):
                ins.append(bias)
        if scale is not None:
            kw["scale"] = scale
            if not isinstance(scale, (int, float)):
                ins.append(scale)
        if accum is not None:
            kw["accum_out"] = accum
            outs.append(accum)
        return self.op(self.act, self.nc.scalar.activation, outs, ins, out=out, in_=a, func=func, **kw)

    def copy(self, out, a, E=None):
        if E is None:
            self.rr += 1
            E = self.dve if self.rr % 2 else self.act
        if E is self.act:
            return self.actf(out, a, AF.Copy)
        return self.op(E, E.eng.tensor_copy, [out], [a], out=out, in_=a)

    def memset(self, out, v, E=None):
        E = E or self.dve
        return self.op(E, E.eng.memset, [out], [], out, v)

    def mm(self, out, lhsT, rhs, start=True, stop=True):
        return self.op(self.pe, self.nc.tensor.matmul, [out], [lhsT, rhs], out, lhsT=lhsT, rhs=rhs,
                       start=start, stop=stop)

    def tr(self, out, a, ident):
        return self.op(self.pe, self.nc.tensor.transpose, [out], [a, ident], out, a, ident)


def _delayed(g, n):
    for _ in range(n):
        yield
    yield from g


def run_gens(items, make, width, stag=0):
    free = list(range(width))
    active = []
    items = list(items)
    idx = 0
    while idx < len(items) or active:
        while idx < len(items) and free:
            sl = free.pop(0)
            g = make(items[idx], sl)
            if stag and idx < width:
                g = _delayed(g, idx * stag)
            active.append((g, sl))
            idx += 1
        for ent in list(active):
            try:
                next(ent[0])
            except StopIteration:
                active.remove(ent)
                free.append(ent[1])


def build(LP):
    DRAM_NAMES.clear()
    kb = KB()
    nc = kb.nc
    es = kb.es
    act, dve, pool, pe = kb.act, kb.dve, kb.pool, kb.pe

    def din(name, shape):
        return kb.dram(name, shape, F32, "ExternalInput")

    def dout(name, shape):
        return kb.dram(name, shape, F32, "ExternalOutput")

    xp = din("xp", [LP, D])
    xs = din("xs", [2, LS, D])
    cin = din("cin", [3, D])
    s5re_in = din("s5re_in", [2, 64, 64])
    s5im_in = din("s5im_in", [2, 64, 64])
    gdn_in = din("gdn_in", [2, 8, 128, 128])
    conv_in = din("conv_in", [2, 3, 2048])
    norm_g = din("norm_g", [2, D])
    w_ada = din("w_ada", [2, D, 3 * D])
    b_ada = din("b_ada", [2, 3 * D])
    s5_w_in = din("s5_w_in", [D, 2 * D])
    s5_log_step = din("s5_log_step", [1, 64])
    s5_lre = din("s5_lre", [64, 64])
    s5_lim = din("s5_lim", [64, 64])
    s5_bre = din("s5_bre", [64, 64, 16])
    s5_bim = din("s5_bim", [64, 64, 16])
    s5_cre = din("s5_cre", [64, 16, 64])
    s5_cim = din("s5_cim", [64, 16, 64])
    s5_d = din("s5_d", [1, D])
    s5_w_glu = din("s5_w_glu", [D, D])
    s5_b_glu = din("s5_b_glu", [1, D])
    s5_w_out = din("s5_w_out", [D, D])
    gdn_w_in = din("gdn_w_in", [D, GIN])
    gdn_conv_w = din("gdn_conv_w", [4, 2048])
    gdn_a_log = din("gdn_a_log", [1, 8])
    gdn_dt_bias = din("gdn_dt_bias", [1, 8])
    gdn_norm_g = din("gdn_norm_g", [1, 128])
    gdn_w_out = din("gdn_w_out", [D, D])
    final_g = din("final_g", [1, D])

    yp = dout("yp", [LP, D])
    ys = dout("ys", [2, LS, D])
    o_s5re_p = dout("o_s5re_p", [64, 64])
    o_s5im_p = dout("o_s5im_p", [64, 64])
    o_gdn_p = dout("o_gdn_p", [8, 128, 128])
    o_conv_p = dout("o_conv_p", [3, 2048])
    o_s5re_s = dout("o_s5re_s", [2, 64, 64])
    o_s5im_s = dout("o_s5im_s", [2, 64, 64])
    o_gdn_s = dout("o_gdn_s", [2, 8, 128, 128])
    o_conv_s = dout("o_conv_s", [2, 3, 2048])

    mods = kb.dram("mods", [2, 3, 3 * D])
    gmod = kb.dram("gmod", [2, 3, D])

    class Seq:
        pass

    seqs = []
    for si, L in enumerate([LP, LS, LS]):
        s = Seq()
        s.i = si
        s.L = L
        s.x = xp if si == 0 else xs[si - 1]
        s.y = yp if si == 0 else ys[si - 1]
        s.uz = kb.dram("uz%d" % si, [L, 2 * D], BF16)
        s.ys5 = kb.dram("ys5_%d" % si, [L, D], BF16)
        s.x1 = kb.dram("x1_%d" % si, [L, D], F32)
        s.proj = kb.dram("proj%d" % si, [L + 3, 2048], F32)
        s.z1 = kb.dram("z1_%d" % si, [L, D], BF16)
        s.ba = kb.dram("ba%d" % si, [L, 16], F32)
        seqs.append(s)

    gs = contextlib.ExitStack()
    es.callback(gs.close)
    ident = kb.sb(gs, [128, 128], F32, "ident")
    ones = kb.sb(gs, [128, 128], F32, "ones")
    psl = [es.enter_context(nc.psum_tensor("ps%d" % i, [128, 512], F32)) for i in range(8)]
    pstate = {"i": 0}

    def PS():
        pstate["i"] = (pstate["i"] + 1) % 8
        return psl[pstate["i"]]

    identb = kb.sb(gs, [128, 128], BF16, "identb")
    epsT = kb.sb(gs, [128, 1], F32, "epsT")
    kb.memset(epsT[:], EPS)
    kb.memset(ones[:], 1.0)
    kb.memset(ident[:], 0.0)
    kb.op(pool, nc.gpsimd.affine_select, [ident[:]], [ones[:]], out=ident[:], in_=ones[:], pattern=[[-1, 128]],
          compare_op=ALU.is_equal, fill=0.0, base=0, channel_multiplier=1)

    kb.copy(identb[:], ident[:], dve)

    def mask_tile(st, name, base, cmp, blk):
        m = kb.sb(st, [128, 128], F32, name)
        kb.op(pool, nc.gpsimd.affine_select, [m[:]], [ones[:]], out=m[:], in_=ones[:], pattern=[[1, 128]],
              compare_op=cmp, fill=0.0, base=base, channel_multiplier=-1)
        if blk:
            kb.memset(m[0:64, 64:128], 0.0, pool)
            kb.memset(m[64:128, 0:64], 0.0, pool)
        return m

    def bcast_load(st, src_row, n, name, dt=F32):
        t = kb.sb(st, [128, n], dt, name)
        kb.dma(t[:], src_row.partition_broadcast(128))
        return t

    stage = [kb.sb(gs, [128, 1024], F32, "stage%d" % i) for i in range(2)]
    stg = {"i": 0}

    def load_w_bf16(dst, src, ncols):
        KC = src.shape[0] // 128
        for kc in range(KC):
            c0 = 0
            while c0 < ncols:
                cw = min(1024, ncols - c0)
                stg["i"] ^= 1
                sgt = stage[stg["i"]]
                kb.dma(sgt[:, :cw], src[kc * 128:(kc + 1) * 128, c0:c0 + cw])
                kb.copy(dst[:, kc, c0:c0 + cw], sgt[:, :cw])
                c0 += cw

    def transposes(src, R, hT, nk, PSf=None):
        for k0 in range(0, nk, 4):
            ps = (PSf or PS)()
            nn = min(4, nk - k0)
            for j in range(nn):
                kb.tr(ps[:, j * 128:j * 128 + R], src[:R, (k0 + j) * 128:(k0 + j + 1) * 128], ident[:R, :R])
            kb.copy(hT[:, k0:k0 + nn, :R], ps[:, :nn * 128].rearrange("p (a b) -> p a b", b=128)[:, :, :R])

    def project(hT, R, W, ncols, sink):
        c0 = 0
        while c0 < ncols:
            cw = min(512, ncols - c0)
            ps = PS()
            for kc in range(8):
                kb.mm(ps[:R, :cw], hT[:, kc, :R], W[:, kc, c0:c0 + cw], start=(kc == 0), stop=(kc == 7))
            sink(c0, cw, ps[:R, :cw])
            c0 += cw

    def project_g(hT, R, W, ncols, sink, PSf=None):
        c0 = 0
        while c0 < ncols:
            cw = min(512, ncols - c0)
            ps = (PSf or PS)()
            for kc in range(8):
                kb.mm(ps[:R, :cw], hT[:, kc, :R], W[:, kc, c0:c0 + cw], start=(kc == 0), stop=(kc == 7))
            sink(c0, cw, ps[:R, :cw])
            c0 += cw
            yield

    def rstd_of(st_tiles, xt, R, width, junk):
        ss, ms, rs = st_tiles
        kb.actf(junk[:R, :width], xt, AF.Square, accum=ss[:R])
        kb.actf(ms[:R], ss[:R], AF.Sqrt, bias=epsT[:R], scale=1.0 / width)
        kb.op(dve, nc.vector.reciprocal, [rs[:R]], [ms[:R]], out=rs[:R], in_=ms[:R])
        return rs

    def norm_mod(xt, R, gm, sh, h, junk, smalls):
        rs = rstd_of(smalls, xt[:R], R, D, junk)
        kb.stt(junk[:R], xt[:R], rs[:R], gm[:R], ALU.mult, ALU.mult)
        kb.tt(pool, h[:R], junk[:R], sh[:R], ALU.add)

    with contextlib.ExitStack() as st:
        ct = kb.sb(st, [3, D], F32, "ct")
        cs = kb.sb(st, [3, D], F32, "cs")
        csT = kb.sb(st, [128, 8, 3], F32, "csT")
        wst = [kb.sb(st, [128, 3 * D], F32, "wada%d" % i) for i in range(2)]
        modt = kb.sb(st, [3, 3 * D], F32, "modt")
        bt = kb.sb(st, [3, 3 * D], F32, "bt")
        ngt = kb.sb(st, [3, D], F32, "ngt")
        gmt = kb.sb(st, [3, D], F32, "gmt")
        kb.dma(ct[:], cin[:, :])
        kb.actf(cs[:], ct[:], AF.Silu)
        ps = PS()
        for kc in range(8):
            kb.tr(ps[:, kc * 4:kc * 4 + 3], cs[:3, kc * 128:(kc + 1) * 128], ident[:3, :3])
        kb.copy(csT[:], ps[:, :32].rearrange("p (a b) -> p a b", b=4)[:, :, :3], dve)
        for layer in range(2):
            pss = [PS() for _ in range(6)]
            for kc in range(8):
                w = wst[kc % 2]
                kb.dma(w[:], w_ada[layer, kc * 128:(kc + 1) * 128, :])
                for n in range(6):
                    kb.mm(pss[n][:3, :], csT[:, kc, :], w[:, n * 512:(n + 1) * 512], start=(kc == 0), stop=(kc == 7))
            kb.dma(bt[:], b_ada[layer:layer + 1, :].partition_broadcast(3))
            for n in range(6):
                kb.tt(dve, modt[:, n * 512:(n + 1) * 512], pss[n][:3, :], bt[:, n * 512:(n + 1) * 512], ALU.add)
            kb.dma(mods[layer], modt[:])
            kb.dma(ngt[:], norm_g[layer:layer + 1, :].partition_broadcast(3))
            kb.stt(gmt[:], modt[:, D:2 * D], 1.0, ngt[:], ALU.add, ALU.mult)
            kb.dma(gmod[layer], gmt[:])
    kb.barrier()

    with contextlib.ExitStack() as st:
        W = kb.sb(st, [128, 8, 2 * D], BF16, "w_in0")
        load_w_bf16(W, s5_w_in, 2 * D)
        B1 = []
        for sl in range(6):
            B1.append((kb.sb(st, [128, D], F32, "xt"), kb.sb(st, [128, D], F32, "h"), kb.sb(st, [128, D], F32, "junk"),
                       kb.sb(st, [128, 8, 128], BF16, "hT"), kb.sb(st, [128, 2 * D], BF16, "uzt"),
                       [kb.sb(st, [128, 1], F32, "sm%d" % i) for i in range(3)]))

        def p1_gen(arg, sl):
            s, t0, gm, sh = arg
            xt, h, junk, hT, uzt, smalls = B1[sl]
            R = min(128, s.L - t0)
            kb.dma(xt[:R], s.x[t0:t0 + R, :])
            norm_mod(xt, R, gm, sh, h, junk, smalls)
            yield
            transposes(h, R, hT, 8)
            yield
            yield from project_g(hT, R, W, 2 * D, lambda c0, cw, p: kb.copy(uzt[:R, c0:c0 + cw], p))
            kb.dma(s.uz[t0:t0 + R, :], uzt[:R])

        items = []
        for s in seqs:
            gm = bcast_load(st, gmod[0, s.i:s.i + 1, :], D, "gm")
            sh = bcast_load(st, mods[0, s.i:s.i + 1, 0:D], D, "sh")
            items += [(s, t0, gm, sh) for t0 in range(0, s.L, 128)]
        run_gens(items, p1_gen, 6, 1)
    kb.barrier()

    with contextlib.ExitStack() as st:
        CT = kb.sb(st, [128, 64, 128], BF16, "CT")
        BS = kb.sb(st, [128, 64, 128], BF16, "BS")
        BSJ = kb.sb(st, [128, 64, 128], BF16, "BSJ")
        T0 = kb.sb(st, [128, 64, 128], BF16, "T0")
        JT = kb.sb(st, [128, 128], F32, "JT")
        AA = kb.sb(st, [128, 2, 2, 64], F32, "AA")
        st_outer = st
        st = contextlib.ExitStack()

        def t64(name, n=64):
            return kb.sb(st, [64, n], F32, name)

        lre, lim, stp = t64("lre"), t64("lim"), t64("stp")
        ldt = kb.sb(st, [128, 64], F32, "ldt")

        def load_T(dst, src, rows):
            kb.dma(ldt[:rows], src)
            ps_ = PS()
            kb.tr(ps_[0:64, 0:rows], ldt[:rows, :], ident[:rows, :rows])
            kb.copy(dst, ps_[0:64, 0:rows], dve)

        load_T(lre[:], s5_lre, 64)
        load_T(lim[:], s5_lim, 64)
        kb.dma(stp[:], s5_log_step[0:1, :].partition_broadcast(64))
        kb.actf(stp[:], stp[:], AF.Exp)
        mag, ph, ar, ai, tmp, tmp2, den = [t64("d%d" % i) for i in range(7)]
        kb.tt(dve, mag[:], lre[:], stp[:], ALU.mult)
        kb.actf(mag[:], mag[:], AF.Exp)
        kb.tt(dve, ph[:], lim[:], stp[:], ALU.mult)

        def sin_of(out, phase, shift):
            f = t64("sf")
            fi = kb.sb(st, [64, 64], mybir.dt.int32, "sfi")
            ff = t64("sff")
            kb.ts(dve, f[:], phase, 1.0 / (2 * np.pi), shift / (2 * np.pi), ALU.mult, ALU.add)
            kb.copy(fi[:], f[:], dve)
            kb.copy(ff[:], fi[:], dve)
            kb.tt(dve, f[:], f[:], ff[:], ALU.subtract)
            kb.ts(dve, f[:], f[:], 0.5, -0.5, ALU.min, ALU.max)
            kb.actf(out, f[:], AF.Sin, scale=2 * np.pi)

        sn, cn = t64("sn"), t64("cn")
        sin_of(sn[:], ph[:], 0.0)
        sin_of(cn[:], ph[:], np.pi / 2)
        kb.tt(dve, ar[:], mag[:], cn[:], ALU.mult)
        kb.tt(dve, ai[:], mag[:], sn[:], ALU.mult)
        nr, ni, xr = t64("nr"), t64("ni"), t64("xr")
        kb.tt(dve, den[:], lre[:], lre[:], ALU.mult)
        kb.tt(dve, tmp[:], lim[:], lim[:], ALU.mult)
        kb.tt(dve, den[:], den[:], tmp[:], ALU.add)
        kb.op(dve, nc.vector.reciprocal, [den[:]], [den[:]], out=den[:], in_=den[:])
        kb.ts(dve, xr[:], ar[:], -1.0, None, ALU.add)
        kb.tt(dve, tmp[:], xr[:], lre[:], ALU.mult)
        kb.tt(dve, tmp2[:], ai[:], lim[:], ALU.mult)
        kb.tt(dve, nr[:], tmp[:], tmp2[:], ALU.add)
        kb.tt(dve, nr[:], nr[:], den[:], ALU.mult)
        kb.tt(dve, tmp[:], ai[:], lre[:], ALU.mult)
        kb.tt(dve, tmp2[:], xr[:], lim[:], ALU.mult)
        kb.tt(dve, ni[:], tmp[:], tmp2[:], ALU.subtract)
        kb.tt(dve, ni[:], ni[:], den[:], ALU.mult)

        def cmul(outr, outi, ar_, ai_, br_, bi_, t1, t2):
            kb.tt(dve, t1, ar_, br_, ALU.mult)
            kb.tt(dve, t2, ai_, bi_, ALU.mult)
            kb.tt(dve, outr, t1, t2, ALU.subtract)
            kb.tt(dve, t1, ar_, bi_, ALU.mult)
            kb.tt(dve, t2, ai_, br_, ALU.mult)
            kb.tt(dve, outi, t1, t2, ALU.add)

        pwr = kb.sb(st, [64, 9, 64], F32, "pwr")
        pwi = kb.sb(st, [64, 9, 64], F32, "pwi")
        ipr = kb.sb(st, [64, 9, 64], F32, "ipr")
        ipi = kb.sb(st, [64, 9, 64], F32, "ipi")
        kb.memset(pwr[:, 0, :], 1.0)
        kb.memset(pwi[:, 0, :], 0.0)
        kb.memset(ipr[:, 0, :], 1.0)
        kb.memset(ipi[:, 0, :], 0.0)
        ivr, ivi = t64("ivr"), t64("ivi")
        kb.tt(dve, tmp[:], mag[:], mag[:], ALU.mult)
        kb.op(dve, nc.vector.reciprocal, [tmp[:]], [tmp[:]], out=tmp[:], in_=tmp[:])
        kb.tt(dve, ivr[:], ar[:], tmp[:], ALU.mult)
        kb.tt(dve, ivi[:], ai[:], tmp[:], ALU.mult)
        kb.ts(dve, ivi[:], ivi[:], -1.0, None, ALU.mult)
        for k_ in range(1, 9):
            cmul(pwr[:, k_, :], pwi[:, k_, :], pwr[:, k_ - 1, :], pwi[:, k_ - 1, :], ar[:], ai[:], tmp[:], tmp2[:])
            cmul(ipr[:, k_, :], ipi[:, k_, :], ipr[:, k_ - 1, :], ipi[:, k_ - 1, :], ivr[:], ivi[:], tmp[:], tmp2[:])

        def t3(name, a, b, dt=F32):
            return kb.sb(st, [64, a, b], dt, name)

        br_, bi_ = t3("br", 64, 16), t3("bi", 64, 16)
        kb.dma(br_[:], s5_bre.rearrange("g p c -> p g c"))
        kb.dma(bi_[:], s5_bim.rearrange("g p c -> p g c"))
        bbr, bbi, w1, w2 = t3("bbr", 64, 16), t3("bbi", 64, 16), t3("w1", 64, 16), t3("w2", 64, 16)

        def bc(a):
            return a.unsqueeze(2).to_broadcast([64, 64, 16])

        cmul(bbr[:], bbi[:], bc(nr[:]), bc(ni[:]), br_[:], bi_[:], w1[:], w2[:])
        cr_, ci_ = t3("cr", 64, 16), t3("ci", 64, 16)
        for g0_ in range(0, 64, 8):
            load_T(cr_[:, g0_:g0_ + 8, :].rearrange("p g c -> p (g c)"),
                   s5_cre[g0_:g0_ + 8].rearrange("g c p -> (g c) p"), 128)
            load_T(ci_[:, g0_:g0_ + 8, :].rearrange("p g c -> p (g c)"),
                   s5_cim[g0_:g0_ + 8].rearrange("g c p -> (g c) p"), 128)

        QT = kb.sb(st, [128, 64, 128], BF16, "QT")
        BST = kb.sb(st, [128, 64, 128], F32, "BST")
        imst = kb.sb(st, [64, 64, 16], F32, "imst")
        rest = kb.sb(st, [64, 64, 16], F32, "rest")
        imst2 = kb.sb(st, [64, 64, 16], BF16, "imst2")
        for s_ in range(8):
            def v(T):
                return T[0:64].rearrange("p g (s c) -> p g s c", c=16)[:, :, s_, :]
            cmul(rest[:], imst[:], bc(ipr[:, 1 + s_, :]), bc(ipi[:, 1 + s_, :]), bbr[:], bbi[:], w1[:], w2[:])
            kb.copy(v(QT), rest[:], dve)
            kb.copy(imst2[:], imst[:], dve)
            kb.dma(QT[64:128].rearrange("p g (s c) -> p g s c", c=16)[:, :, s_, :], imst2[:])
            cmul(rest[:], imst[:], bc(pwr[:, 7 - s_, :]), bc(pwi[:, 7 - s_, :]), bbr[:], bbi[:], w1[:], w2[:])
            kb.copy(v(BST), rest[:], dve)
            kb.dma(BST[64:128].rearrange("p g (s c) -> p g s c", c=16)[:, :, s_, :], imst[:])
            cmul(rest[:], imst[:], bc(pwr[:, 1 + s_, :]), bc(pwi[:, 1 + s_, :]), cr_[:], ci_[:], w1[:], w2[:])
            kb.copy(v(CT), rest[:], dve)
            kb.ts(dve, imst2[:], imst[:], -1.0, None, ALU.mult)
            kb.dma(CT[64:128].rearrange("p g (s c) -> p g s c", c=16)[:, :, s_, :], imst2[:])

        kb.memset(JT[:], 0.0)
        kb.dma(JT[0:64, 64:128], ident[0:64, 0:64])
        negI = kb.sb(st, [64, 64], F32, "negI")
        kb.ts(dve, negI[:], ident[0:64, 0:64], -1.0, None, ALU.mult)
        kb.dma(JT[64:128, 0:64], negI[:])

        tmask = kb.sb(st, [128, 128], F32, "tmask")
        kb.memset(tmask[:], 1.0)
        for s_ in range(1, 8):
            kb.memset(tmask[s_ * 16:(s_ + 1) * 16 if False else 128, 0:s_ * 16], 0.0) if False else None
        kb.op(pool, nc.gpsimd.affine_select, [tmask[:]], [ones[:]],
              out=tmask[:].rearrange("p (t c) -> p t c", c=16), in_=ones[:].rearrange("p (t c) -> p t c", c=16),
              pattern=[[16, 8], [0, 16]], compare_op=ALU.is_ge, fill=0.0, base=15, channel_multiplier=-1)
        bsj32 = kb.sb(st, [128, 512], F32, "bsj32")
        for g0 in range(0, 64, 4):
            ps = PS()
            for j in range(4):
                kb.tr(ps[:, j * 128:(j + 1) * 128], BST[:, g0 + j, :], ident[:])
            kb.copy(BS[:, g0:g0 + 4, :], ps[:].rearrange("p (a b) -> p a b", b=128))
            ps2 = PS()
            kb.mm(ps2[:], JT[:], BST[:, g0:g0 + 4, :].rearrange("p a b -> p (a b)"))
            kb.copy(bsj32[:], ps2[:])
            ps3 = PS()
            for j in range(4):
                kb.tr(ps3[:, j * 128:(j + 1) * 128], bsj32[:, j * 128:(j + 1) * 128], ident[:])
            kb.copy(BSJ[:, g0:g0 + 4, :], ps3[:].rearrange("p (a b) -> p a b", b=128))
            ps4 = PS()
            for j in range(4):
                kb.mm(ps4[:, j * 128:(j + 1) * 128], QT[:, g0 + j, :], CT[:, g0 + j, :])
            kb.tt(dve, T0[:, g0:g0 + 4, :], ps4[:].rearrange("p (a b) -> p a b", b=128),
                  tmask[:].unsqueeze(1).to_broadcast([128, 4, 128]), ALU.mult)

        A8r = kb.sb(st, [128, 64], F32, "A8r")
        A8i = kb.sb(st, [128, 64], F32, "A8i")
        kb.copy(A8r[0:64], pwr[:, 8, :], dve)
        kb.copy(A8i[0:64], pwi[:, 8, :], dve)
        kb.dma(A8r[64:128], pwr[:, 8, :])
        kb.dma(A8i[64:128], pwi[:, 8, :])
        kb.copy(AA[:, 0, 0, :], A8r[:], dve)
        kb.copy(AA[:, 0, 1, :], A8r[:], dve)
        kb.copy(AA[:, 1, 0, :], A8i[:], dve)
        kb.ts(dve, AA[:, 1, 1, :], A8i[:], -1.0, None, ALU.mult)

        kb.barrier()
        st.close()
        st = st_outer
        NM = 64
        usc = kb.sb(st, [NM, 8, D], BF16, "usc")
        usc2 = kb.sb(st, [NM, 64, 128], BF16, "usc2")
        UTa = kb.sb(st, [128, 64, NM], BF16, "UTa")
        UTb = [stage[0][:].bitcast(BF16).rearrange("p (g m) -> p g m", m=NM),
               stage[1][:].bitcast(BF16).rearrange("p (g m) -> p g m", m=NM)]
        SSs = [kb.sb(st, [128, NM, 2, 64], BF16, "SS%d" % i) for i in range(2)]
        ZB = kb.sb(st, [128, NM + 1, 2, 64], F32, "ZB")
        XH = kb.sb(st, [128, 64, NM], BF16, "XH")
        ysc = kb.sb(st, [NM, 8, D], BF16, "ysc")
        W4 = [kb.sb(st, [128, 4, 64], F32, "W4%d" % i) for i in range(2)]
        TT = kb.sb(st, [128, 2, 2, 64], F32, "TT")
        UU = kb.sb(st, [128, 2, 64], F32, "UU")

        def wview(t):
            a_ = t[:, 0:2, :]
            return bass.AP(a_.tensor, a_.offset, [[a_.ap[0][0], 128], [64, 2], [64, 2], [1, 64]])

        def UTg(sl, g, nm):
            if sl == 0:
                return UTa[:, g, :nm]
            return UTb[g // 32][:, g % 32, :nm]

        def UTg4(sl, g0, nm):
            if sl == 0:
                return UTa[:, g0:g0 + 4, :nm]
            return UTb[g0 // 32][:, g0 % 32:g0 % 32 + 4, :nm]

        xin = kb.sb(st, [64, 128], F32, "xin")
        rstate = {"done": 0, "pre": 0, "post": 0}

        def s5_tile(arg, sl):
            s, ti, m0, nm = arg
            SS = SSs[sl]
            tok0 = m0 * 8
            while rstate["pre"] < ti:
                yield
            kb.dma(usc[:nm], s.uz[tok0:tok0 + nm * 8, 0:D].rearrange("(m s) e -> m s e", s=8))
            for hh_ in range(2):
                kb.copy(usc2[:nm, hh_ * 32:(hh_ + 1) * 32, :].rearrange("m g (s c) -> m g s c", c=16),
                        usc[:nm, :, hh_ * 512:(hh_ + 1) * 512].rearrange("m s (g c) -> m g s c", c=16), act)
                yield
            for g0 in range(0, 64, 4):
                ps = PS()
                for j in range(4):
                    g = g0 + j
                    kb.tr(ps[:].bitcast(BF16)[:, j * 128:j * 128 + nm],
                          usc2[:nm, g, :], identb[:nm, :nm])
                kb.copy(UTg4(sl, g0, nm),
                        ps[:].bitcast(BF16)[:, 0:512].rearrange("p (a b) -> p a b", b=128)[:, :, :nm], act)
                if g0 == 60:
                    rstate["pre"] = ti + 1
                yield
            for g0 in range(0, 64, 4):
                ps = PS()
                ps2 = PS()
                for j in range(4):
                    g = g0 + j
                    kb.mm(ps[:, j * 128:j * 128 + nm], BS[:, g, :], UTg(sl, g, nm))
                    kb.mm(ps2[:, j * 128:j * 128 + nm], BSJ[:, g, :], UTg(sl, g, nm))
                kb.copy(SS[:, :nm, 0, g0:g0 + 4].rearrange("p m g -> p g m"),
                        ps[:].rearrange("p (a b) -> p a b", b=128)[:, :, :nm], act)
                kb.copy(SS[:, :nm, 1, g0:g0 + 4].rearrange("p m g -> p g m"),
                        ps2[:].rearrange("p (a b) -> p a b", b=128)[:, :, :nm], act)
                yield
            while rstate["done"] < ti:
                yield
            kb.copy(W4[0][:, 0:2, :], ZB[:, 0, :, :], dve)
            kb.copy(W4[0][:, 2:4, :], ZB[:, 0, :, :], dve)
            for m in range(nm):
                cur, nxt = W4[m % 2], W4[(m + 1) % 2]
                kb.tt(dve, TT[:], wview(cur), AA[:], ALU.mult)
                kb.tt(dve, UU[:], TT[:, 0, :, :], TT[:, 1, :, :], ALU.add)
                kb.tt(dve, nxt[:].rearrange("p (r b) c -> p r b c", r=2),
                      UU[:].unsqueeze(1).to_broadcast([128, 2, 2, 64]),
                      SS[:, m, :, :].unsqueeze(1).to_broadcast([128, 2, 2, 64]), ALU.add)
                kb.copy(ZB[:, m + 1, :, :], nxt[:, 0:2, :], pool)
                yield
            while rstate["post"] < ti:
                yield
            kb.copy(XH[:, :, :nm], ZB[:, 0:nm, 0, :].rearrange("p m g -> p g m"), pool)
            kb.copy(ZB[:, 0, :, :], ZB[:, nm, :, :], dve)
            rstate["done"] = ti + 1
            yield
            for g0 in range(0, 64, 4):
                ps = PS()
                for j in range(4):
                    g = g0 + j
                    kb.mm(ps[:nm, j * 128:(j + 1) * 128], UTg(sl, g, nm), T0[:, g, :], start=True, stop=False)
                    kb.mm(ps[:nm, j * 128:(j + 1) * 128], XH[:, g, :nm], CT[:, g, :], start=False, stop=True)
                kb.copy(ysc[:nm, :, g0 * 16:(g0 + 4) * 16].rearrange("m t (g c) -> m g t c", c=16),
                        ps[:nm, :].rearrange("m (g t c) -> m g t c", g=4, c=16), act)
                yield
            kb.dma(s.ys5[tok0:tok0 + nm * 8, :].rearrange("(m s) e -> m s e", s=8), ysc[:nm])
            rstate["post"] = ti + 1

        for s in seqs:
            if s.i == 0:
                kb.memset(ZB[:, 0, :, :], 0.0)
            else:
                kb.dma(xin[:, 0:64], s5re_in[s.i - 1])
                kb.dma(xin[:, 64:128], s5im_in[s.i - 1])
                ps = PS()
                kb.tr(ps[:, 0:64], xin[:], ident[0:64, 0:64])
                kb.copy(ZB[:, 0, 0, :], ps[:, 0:64], dve)
                ps = PS()
                kb.mm(ps[:, 0:64], JT[:], ZB[:, 0, 0, :])
                kb.copy(ZB[:, 0, 1, :], ps[:, 0:64], dve)
            nsub = s.L // 8
            rstate["done"] = 0
            rstate["pre"] = 0
            rstate["post"] = 0
            items = [(s, ti, m0, min(NM, nsub - m0)) for ti, m0 in enumerate(range(0, nsub, NM))]
            run_gens(items, s5_tile, 2)
            ps = PS()
            kb.tr(ps[0:64, 0:128], ZB[:, 0, 0, :], ident[:])
            kb.copy(xin[:], ps[0:64, 0:128], dve)
            ore = o_s5re_p if s.i == 0 else o_s5re_s[s.i - 1]
            oim = o_s5im_p if s.i == 0 else o_s5im_s[s.i - 1]
            kb.dma(ore, xin[:, 0:64])
            kb.dma(oim, xin[:, 64:128])
    kb.barrier()

    with contextlib.ExitStack() as st:
        Wg = kb.sb(st, [128, 8, D], BF16, "w_glu")
        Wo = kb.sb(st, [128, 8, D], BF16, "w_out0")
        load_w_bf16(Wg, s5_w_glu, D)
        load_w_bf16(Wo, s5_w_out, D)
        dB = bcast_load(st, s5_d[0:1, :], D, "dB")
        bgB = bcast_load(st, s5_b_glu[0:1, :], D, "bgB")
        B3 = []
        for sl in range(5):
            B3.append(dict(yt=kb.sb(st, [128, D], BF16, "yt"), uzt=kb.sb(st, [128, 2 * D], BF16, "uzt3"),
                           xt=kb.sb(st, [128, D], F32, "xt3"), a=kb.sb(st, [128, D], F32, "a3"),
                           b=kb.sb(st, [128, D], F32, "b3"), y2=kb.sb(st, [128, D], F32, "y2"),
                           sg=kb.sb(st, [128, D], F32, "sg"), hT=kb.sb(st, [128, 8, 128], BF16, "hT3")))

        def p3_gen(arg, sl):
            s, t0, gt = arg
            w = B3[sl]
            yt, uzt, xt, a, b, y2, sg, hT = w["yt"], w["uzt"], w["xt"], w["a"], w["b"], w["y2"], w["sg"], w["hT"]
            R = min(128, s.L - t0)
            kb.dma(yt[:R], s.ys5[t0:t0 + R, :])
            kb.dma(uzt[:R], s.uz[t0:t0 + R, :])
            kb.dma(xt[:R], s.x[t0:t0 + R, :])
            kb.tt(dve, a[:R], uzt[:R, 0:D], dB[:R], ALU.mult)
            kb.tt(pool, a[:R], a[:R], yt[:R], ALU.add)
            kb.actf(y2[:R], a[:R], AF.Gelu_apprx_tanh)
            yield
            transposes(y2, R, hT, 8)
            yield
            yield from project_g(hT, R, Wg, D, lambda c0, cw, p: kb.tt(dve, sg[:R, c0:c0 + cw], p, bgB[:R, c0:c0 + cw], ALU.add))
            kb.actf(sg[:R], sg[:R], AF.Sigmoid)
            kb.tt(dve, y2[:R], y2[:R], sg[:R], ALU.mult)
            kb.actf(b[:R], uzt[:R, D:2 * D], AF.Silu)
            kb.tt(pool, y2[:R], y2[:R], b[:R], ALU.mult)
            yield
            transposes(y2, R, hT, 8)
            yield
            yield from project_g(hT, R, Wo, D, lambda c0, cw, p: kb.tt(dve, a[:R, c0:c0 + cw], p, gt[:R, c0:c0 + cw], ALU.mult))
            kb.tt(pool, a[:R], a[:R], xt[:R], ALU.add)
            kb.dma(s.x1[t0:t0 + R, :], a[:R])

        items = []
        for s in seqs:
            gt = bcast_load(st, mods[0, s.i:s.i + 1, 2 * D:3 * D], D, "gate0")
            items += [(s, t0, gt) for t0 in range(0, s.L, 128)]
        run_gens(items, p3_gen, 5, 2)
    kb.barrier()

    with contextlib.ExitStack() as st:
        W = kb.sb(st, [128, 8, GIN], BF16, "w_in1")
        load_w_bf16(W, gdn_w_in, GIN)
        zero3 = kb.sb(st, [3, 2048], F32, "zero3")
        kb.memset(zero3[:], 0.0)
        B4 = []
        for sl in range(4):
            B4.append(dict(xt=kb.sb(st, [128, D], F32, "xt4"), h=kb.sb(st, [128, D], F32, "h4"),
                           junk=kb.sb(st, [128, D], F32, "junk4"), hT=kb.sb(st, [128, 8, 128], BF16, "hT4"),
                           qkv=kb.sb(st, [128, 2048], F32, "qkv"), zt=kb.sb(st, [128, D], BF16, "zt4"),
                           bat=kb.sb(st, [128, 16], F32, "bat"),
                           smalls=[kb.sb(st, [128, 1], F32, "sm4%d" % i) for i in range(3)]))

        def p4_gen(arg, sl):
            s, t0, gm, sh = arg
            w = B4[sl]
            xt, h, junk, hT, qkv, zt, bat, smalls = (w["xt"], w["h"], w["junk"], w["hT"], w["qkv"], w["zt"],
                                                     w["bat"], w["smalls"])
            R = min(128, s.L - t0)
            kb.dma(xt[:R], s.x1[t0:t0 + R, :])
            norm_mod(xt, R, gm, sh, h, junk, smalls)
            yield
            transposes(h, R, hT, 8)
            yield

            def sink(c0, cw, p):
                if c0 < 2048:
                    kb.copy(qkv[:R, c0:c0 + cw], p)
                elif c0 < 3072:
                    kb.copy(zt[:R, c0 - 2048:c0 - 2048 + cw], p)
                else:
                    kb.copy(bat[:R, :cw], p, dve)
            yield from project_g(hT, R, W, GIN, sink)
            kb.dma(s.proj[3 + t0:3 + t0 + R, :], qkv[:R])
            kb.dma(s.z1[t0:t0 + R, :], zt[:R])
            kb.dma(s.ba[t0:t0 + R, :], bat[:R])

        items = []
        for s in seqs:
            gm = bcast_load(st, gmod[1, s.i:s.i + 1, :], D, "gm4")
            sh = bcast_load(st, mods[1, s.i:s.i + 1, 0:D], D, "sh4")
            if s.i == 0:
                kb.dma(s.proj[0:3, :], zero3[:])
            else:
                kb.dma(s.proj[0:3, :], conv_in[s.i - 1])
            items += [(s, t0, gm, sh) for t0 in range(0, s.L, 128)]
        run_gens(items, p4_gen, 4, 2)
    kb.barrier()

    with contextlib.ExitStack() as st:
        Wo = kb.sb(st, [128, 8, D], BF16, "w_out1")
        load_w_bf16(Wo, gdn_w_out, D)
        cw = [bcast_load(st, gdn_conv_w[j:j + 1, :], 2048, "cw%d" % j) for j in range(4)]
        alB = bcast_load(st, gdn_a_log[0:1, :], 8, "alB")
        dtB = bcast_load(st, gdn_dt_bias[0:1, :], 8, "dtB")
        gnB = bcast_load(st, gdn_norm_g[0:1, :], 128, "gnB")
        fgB = bcast_load(st, final_g[0:1, :], D, "fgB")
        kb.actf(alB[:], alB[:], AF.Exp)
        kb.ts(dve, alB[:], alB[:], -1.0, None, ALU.mult)
        MU = mask_tile(st, "MU", -1, ALU.is_ge, True)
        MUI = mask_tile(st, "MUI", 0, ALU.is_ge, True)
        MUneg = kb.sb(st, [128, 128], F32, "MUneg")
        kb.ts(dve, MUneg[:], MU[:], -1.0, 30000.0, ALU.add, ALU.mult)
        LC = MUI
        OB = kb.sb(st, [128, 128], F32, "OB")
        kb.memset(OB[:], 1.0)
        kb.memset(OB[0:64, 64:128], 0.0)
        kb.memset(OB[64:128, 0:64], 0.0)
        SEL = [kb.sb(st, [128, 128], F32, "SEL%d" % c) for c in range(2)]
        for c in range(2):
            kb.memset(SEL[c][:], 0.0)
            kb.memset(SEL[c][c * 64:(c + 1) * 64, :], 1.0)

        F = [kb.sb(st, [128, 2048], F32, "F%d" % j) for j in range(4)]
        bat = kb.sb(st, [128, 16], F32, "bat5")
        zts = [kb.sb(st, [128, D], BF16, "zt5") for _ in range(3)]
        x1ts = [kb.sb(st, [128, D], F32, "x1t") for _ in range(3)]
        gt = kb.sb(st, [128, D], F32, "gate1")
        PBs = []
        for i_ in range(2):
            pb = dict(actt=kb.sb(st, [128, 2048], F32, "actt"),
                      kT=kb.sb(st, [128, 4, 128], BF16, "kT"), qT=kb.sb(st, [128, 4, 128], BF16, "qT"),
                      KK=kb.sb(st, [128, 4, 128], F32, "KK"), QK=kb.sb(st, [128, 4, 128], F32, "QK"),
                      egtB=[kb.sb(st, [128, 8], F32, "egtB") for c in range(2)],
                      eglc=[kb.sb(st, [128, 8], F32, "eglc") for c in range(2)],
                      oth=[kb.sb(st, [128, 128], F32, "oth") for h_ in range(8)])
            for nm_ in ["beta", "nbeta", "gg", "gcum", "ngcum", "eg", "egl", "sq", "rq"]:
                pb[nm_] = kb.sb(st, [128, 8], F32, nm_)
            PBs.append(pb)
        sqE = kb.sb(st, [128, 8], F32, "sqE")
        rqE = kb.sb(st, [128, 8], F32, "rqE")
        junkP = kb.sb(st, [128, 128], F32, "junkP")
        junk = kb.sb(st, [128, D], F32, "junk5")
        NSL = 4
        NBK = 2
        STAG = 4
        sps = [0] * NSL
        W_ = []
        for sl in range(NSL):
            w = {}
            for nm_ in ["Rm0", "Rm1"]:
                w[nm_] = kb.sb(st, [128, 128], F32, "%s_%d" % (nm_, sl))
            for i_ in range(2):
                w["AB%d" % i_] = kb.sb(st, [128, 256], F32, "AB%d_%d" % (i_, sl))
                w["Am%d" % i_] = w["AB%d" % i_][:, 0:128]
                w["Bm%d" % i_] = w["AB%d" % i_][:, 128:256]
            w["UW"] = kb.sb(st, [128, 256], F32, "UW_%d" % sl)
            w["dg"] = w["Am1"]
            w["DT"] = w["Bm1"]
            w["DTU"] = w["Rm1"]
            w["qe"] = w["UW"][:, 0:128]
            for nm_ in ["QKD", "Rb", "wT", "qeT", "kel0", "kel1", "vnew"]:
                w[nm_] = kb.sb(st, [128, 128], BF16, "%s_%d" % (nm_, sl))
            w["VK"] = kb.sb(st, [128, 256], BF16, "VK_%d" % sl)
            kb.memset(w["vnew"][:], 0.0)
            W_.append(w)

        def slot_ps(sl):
            def f():
                sps[sl] = (sps[sl] + 1) % NBK
                return psl[sl * NBK + sps[sl]]
            return f
        S = [kb.sb(st, [128, 128], F32, "S%d" % h_) for h_ in range(8)]
        Sb = [kb.sb(st, [128, 128], BF16, "Sb%d" % h_) for h_ in range(8)]
        ot = kb.sb(st, [128, D], F32, "ot")
        o2 = kb.sb(st, [128, D], F32, "o2")
        hT = kb.sb(st, [128, 8, 128], BF16, "hT5")
        smalls = [kb.sb(st, [128, 1], F32, "sm5%d" % i) for i in range(3)]


        def head_gen(hv, sl, R, chunks, pb):
            w = W_[sl]
            actt, KK, QK, oth = pb["actt"], pb["KK"], pb["QK"], pb["oth"]
            gcum, eg, beta, nbeta = pb["gcum"], pb["eg"], pb["beta"], pb["nbeta"]
            ngcum = pb["ngcum"]
            egtB, eglc = pb["egtB"], pb["eglc"]
            dg, DT, DTU, QKD, qe = w["dg"], w["DT"], w["DTU"], w["QKD"], w["qe"]
            Am, Bm, Rm = [w["Am0"], w["Am1"]], [w["Bm0"], w["Bm1"]], [w["Rm0"], w["Rm1"]]
            Rb, wT, qeT, vnew, VK, UW = w["Rb"], w["wT"], w["qeT"], w["vnew"], w["VK"], w["UW"]
            kelc = [w["kel0"], w["kel1"]]
            hq = hv // 2

            def rg(r, n=1):
                sps[sl] = (sps[sl] + 1) % NBK
                return psl[sl * NBK + sps[sl]][:, 0:n * 128]
            kh = actt[:R, 512 + hq * 128:512 + (hq + 1) * 128]
            qh = actt[:R, hq * 128:(hq + 1) * 128]
            vh = actt[:R, 1024 + hv * 128:1024 + (hv + 1) * 128]
            kb.ts(dve, dg[:R, :R], ident[:R, :R], gcum[:R, hv:hv + 1], None, ALU.mult, sins=[gcum[:R]])
            ps = rg(0)
            kb.mm(ps[:R, :R], ones[:R, :R], dg[:R, :R], start=True, stop=False)
            kb.mm(ps[:R, :R], ident[:R, :R], MUneg[:R, :R], start=False, stop=True)
            yield
            kb.actf(DTU[:R, :R], ps[:R, :R], AF.Exp, bias=ngcum[:R, hv:hv + 1])
            kb.tt(pool, DT[:R, :R], DTU[:R, :R], ident[:R, :R], ALU.add)
            kb.tt(pool, QKD[:R, :R], DT[:R, :R], QK[:R, hq, :R], ALU.mult)
            kb.stt(Am[0][:R, :R], KK[:R, hq, :R], nbeta[:R, hv:hv + 1], DTU[:R, :R], ALU.mult, ALU.mult)
            ps = rg(1)
            kb.tr(ps[:R, :R], Am[0][:R, :R], ident[:R, :R])
            kb.copy(VK[:R, 0:128], vh, act)
            kb.ts(pool, VK[:R, 128:256], kh, eg[:R, hv:hv + 1], 1.0, ALU.mult, ALU.mult, sins=[eg[:R]])
            kb.ts(pool, qe[:R, :], qh, eg[:R, hv:hv + 1], 1.0, ALU.mult, ALU.mult, sins=[eg[:R]])
            for c in range(len(chunks)):
                kb.ts(pool, kelc[c][:R, :], kh, eglc[c][:R, hv:hv + 1], 1.0, ALU.mult, ALU.mult, sins=[eglc[c][:R]])
            yield
            kb.copy(Bm[0][:R, :R], ps[:R, :R], act)
            kb.tt(dve, Rm[0][:R, :R], Am[0][:R, :R], ident[:R, :R], ALU.add)
            ps = rg(2)
            kb.tr(ps[:, :R], qe[:R, :], ident[:R, :R])
            yield
            kb.copy(qeT[:, :R], ps[:, :R], act)
            ca, cr = 0, 0
            nlev = 5 if R > 16 else 3
            AB = [w["AB0"], w["AB1"]]
            for lev in range(nlev + 1):
                sq = lev < nlev
                fullsq = lev < nlev - 1
                if sq:
                    psab = rg(3, 2)
                    kb.mm(psab[:R, 128:128 + R], Am[ca][:R, :R], Bm[ca][:R, :R])
                    if fullsq:
                        kb.mm(psab[:R, 0:R], Bm[ca][:R, :R], Am[ca][:R, :R])
                if lev >= 1:
                    psr = rg(1)
                    kb.mm(psr[:R, :R], Bm[ca][:R, :R], Rm[cr][:R, :R])
                yield
                if sq:
                    if fullsq:
                        kb.copy(AB[1 - ca][:R].rearrange("p (a b) -> p a b", b=128)[:, :, :R],
                                psab[:R].rearrange("p (a b) -> p a b", b=128)[:, :, :R], act)
                    else:
                        kb.copy(Bm[1 - ca][:R, :R], psab[:R, 128:128 + R], act)
                if lev >= 1:
                    if lev == nlev:
                        kb.tt(dve, Rb[:R, :R], Rm[cr][:R, :R], psr[:R, :R], ALU.add)
                    else:
                        kb.tt(dve, Rm[1 - cr][:R, :R], Rm[cr][:R, :R], psr[:R, :R], ALU.add)
                        cr = 1 - cr
                ca = 1 - ca
            ps = rg(2, 2)
            kb.mm(ps[:R, 0:256], Rb[:R, :R], VK[:R, :])
            yield
            kb.ts(dve, UW[:R, :], ps[:R, 0:256], beta[:R, hv:hv + 1], None, ALU.mult, sins=[beta[:R]])
            ps = rg(0)
            kb.tr(ps[:, :R], UW[:R, 128:256], ident[:R, :R])
            yield
            kb.copy(wT[:, :R], ps[:, :R], act)
            for c, (r0, r1) in enumerate(chunks):
                ps = rg(1)
                kb.mm(ps[:R, 0:128], wT[:, :R], Sb[hv][:, :])
                ps2 = rg(0)
                kb.mm(ps2[:R, 0:128], qeT[:, :R], Sb[hv][:, :], start=True, stop=False)
                yield
                kb.tt(dve, vnew[r0:r1, :], UW[r0:r1, 0:128], ps[r0:r1, 0:128], ALU.subtract)
                kb.mm(ps2[:R, 0:128], QKD[:R, :R], vnew[:R, :], start=False, stop=True)
                ps3 = rg(2)
                kb.mm(ps3[:, 0:128], kelc[c][:R, :], vnew[:R, :])
                yield
                kb.copy(oth[hv][r0:r1, :], ps2[r0:r1, 0:128], act)
                kb.stt(Sb[hv][:, :], S[hv][:, :], egtB[c][:, hv:hv + 1], ps3[:, 0:128], ALU.mult, ALU.add)
                kb.stt(S[hv][:, :], S[hv][:, :], egtB[c][:, hv:hv + 1], ps3[:, 0:128], ALU.mult, ALU.add)
                yield

        def prologue_gen(s, t0, pb, sl):
            PSP = slot_ps(sl)
            R = min(128, s.L - t0)
            chunks = [(0, min(64, R))] + ([(64, 128)] if R == 128 else [])
            actt, kT, qT, KK, QK = pb["actt"], pb["kT"], pb["qT"], pb["KK"], pb["QK"]
            sq, rq, beta, nbeta, gg, gcum, eg, egl = (pb["sq"], pb["rq"], pb["beta"], pb["nbeta"], pb["gg"],
                                                      pb["gcum"], pb["eg"], pb["egl"])
            egtB, eglc = pb["egtB"], pb["eglc"]
            kb.tt(dve, actt[:R], F[0][:R], cw[0][:R], ALU.mult)
            kb.tt(pool, F[1][:R], F[1][:R], cw[1][:R], ALU.mult)
            yield
            kb.tt(dve, F[2][:R], F[2][:R], cw[2][:R], ALU.mult)
            kb.tt(pool, F[3][:R], F[3][:R], cw[3][:R], ALU.mult)
            yield
            kb.tt(dve, actt[:R], actt[:R], F[1][:R], ALU.add)
            kb.tt(pool, F[2][:R], F[2][:R], F[3][:R], ALU.add)
            yield
            kb.tt(dve, actt[:R], actt[:R], F[2][:R], ALU.add)
            kb.actf(actt[:R], actt[:R], AF.Silu)
            yield
            for hh in range(8):
                kb.actf(junkP[:R, 0:128], actt[:R, hh * 128:(hh + 1) * 128], AF.Square, accum=sq[:R, hh:hh + 1])
                if hh % 4 == 3:
                    yield
            kb.ts(dve, sq[:R], sq[:R], EPS, None, ALU.add)
            kb.actf(sq[:R], sq[:R], AF.Sqrt)
            kb.op(dve, nc.vector.reciprocal, [rq[:R]], [sq[:R]], out=rq[:R], in_=sq[:R])
            kb.ts(dve, rq[:R, 0:4], rq[:R, 0:4], 128.0 ** -0.5, None, ALU.mult)
            yield
            for hh in range(8):
                kb.ts(dve, actt[:R, hh * 128:(hh + 1) * 128], actt[:R, hh * 128:(hh + 1) * 128],
                      rq[:R, hh:hh + 1], None, ALU.mult, sins=[rq[:R]])
                if hh % 2 == 1:
                    yield
            kb.actf(beta[:R], bat[:R, 0:8], AF.Sigmoid)
            kb.ts(dve, nbeta[:R], beta[:R], -1.0, None, ALU.mult)
            kb.tt(dve, gg[:R], bat[:R, 8:16], dtB[:R], ALU.add)
            kb.actf(gg[:R], gg[:R], AF.Exp)
            kb.actf(gg[:R], gg[:R], AF.Ln, bias=1.0)
            kb.tt(dve, gg[:R], gg[:R], alB[:R], ALU.mult)
            yield
            ps = PSP()
            kb.mm(ps[:R, 0:8], LC[:R, :R], gg[:R])
            kb.mm(ps[:R, 8:16], OB[:R, :R], gg[:R])
            kb.copy(gcum[:R], ps[:R, 0:8], dve)
            kb.ts(dve, pb["ngcum"][:R], ps[:R, 0:8], -1.0, None, ALU.mult)
            kb.actf(eg[:R], ps[:R, 0:8], AF.Exp)
            kb.tt(dve, egl[:R], ps[:R, 8:16], gcum[:R], ALU.subtract)
            kb.actf(egl[:R], egl[:R], AF.Exp)
            for c in range(len(chunks)):
                kb.ts(dve, eglc[c][:R], egl[:R], OB[:R, 64 * c:64 * c + 1], None, ALU.mult, sins=[OB[:R]])
            yield
            for c, (r0, r1) in enumerate(chunks):
                ps = PSP()
                kb.mm(ps[:, 0:8], SEL[c][:R, :], gg[:R])
                kb.actf(egtB[c][:], ps[:, 0:8], AF.Exp)
            yield
            ps = PSP()
            for hq in range(4):
                kb.tr(ps[:, hq * 128:hq * 128 + R], actt[:R, 512 + hq * 128:512 + (hq + 1) * 128], ident[:R, :R])
            kb.copy(kT[:, :, :R], ps[:].rearrange("p (a b) -> p a b", b=128)[:, :, :R], act)
            yield
            ps = PSP()
            for hq in range(4):
                kb.tr(ps[:, hq * 128:hq * 128 + R], actt[:R, hq * 128:(hq + 1) * 128], ident[:R, :R])
            kb.copy(qT[:, :, :R], ps[:].rearrange("p (a b) -> p a b", b=128)[:, :, :R], act)
            yield
            ps = PSP()
            for hq in range(4):
                kb.mm(ps[:R, hq * 128:hq * 128 + R], kT[:, hq, :R], kT[:, hq, :R])
            kb.copy(KK[:R, :, :R], ps[:R].rearrange("p (a b) -> p a b", b=128)[:, :, :R], act)
            yield
            ps = PSP()
            for hq in range(4):
                kb.mm(ps[:R, hq * 128:hq * 128 + R], kT[:, hq, :R], qT[:, hq, :R])
            kb.copy(QK[:R, :, :R], ps[:R].rearrange("p (a b) -> p a b", b=128)[:, :, :R], act)
            yield

        def epilogue_gen(s, t0, pb, zt, x1t, sl):
            PSE = slot_ps(sl)
            R = min(128, s.L - t0)
            oth = pb["oth"]
            for hv in range(8):
                kb.actf(junkP[:R, 0:128], oth[hv][:R, :], AF.Square, accum=sqE[:R, hv:hv + 1])
                if hv % 4 == 3:
                    yield
            kb.ts(dve, sqE[:R], sqE[:R], 1.0 / 128, EPS, ALU.mult, ALU.add)
            kb.actf(sqE[:R], sqE[:R], AF.Sqrt)
            kb.op(dve, nc.vector.reciprocal, [rqE[:R]], [sqE[:R]], out=rqE[:R], in_=sqE[:R])
            yield
            for hv in range(8):
                kb.stt(o2[:R, hv * 128:(hv + 1) * 128], oth[hv][:R, :], rqE[:R, hv:hv + 1],
                       gnB[:R, :], ALU.mult, ALU.mult)
                if hv % 2 == 1:
                    yield
            kb.actf(junk[:R], zt[:R], AF.Silu)
            kb.tt(pool, o2[:R], o2[:R], junk[:R], ALU.mult)
            yield
            for k0 in range(0, 8, 4):
                ps = PSE()
                for j in range(4):
                    kb.tr(ps[:, j * 128:j * 128 + R], o2[:R, (k0 + j) * 128:(k0 + j + 1) * 128], ident[:R, :R])
                kb.copy(hT[:, k0:k0 + 4, :R], ps[:, :512].rearrange("p (a b) -> p a b", b=128)[:, :, :R], act)
                yield
            yield from project_g(hT, R, Wo, D, lambda c0, cw_, p: kb.tt(dve, ot[:R, c0:c0 + cw_], p,
                                                                       gt[:R, c0:c0 + cw_], ALU.mult), PSE)
            kb.tt(pool, ot[:R], ot[:R], x1t[:R], ALU.add)
            yield
            rs = rstd_of(smalls, ot[:R], R, D, junk)
            yield
            kb.stt(o2[:R], ot[:R], rs[:R], fgB[:R], ALU.mult, ALU.mult)
            kb.dma(s.y[t0:t0 + R, :], o2[:R])

        def drain(g):
            for _ in g:
                pass

        def delayed(g, n):
            for _ in range(n):
                yield
            yield from g

        def run_tile(jobs):
            free = list(range(NSL))
            active = []
            idx = 0
            while idx < len(jobs) or active:
                while idx < len(jobs) and free:
                    sl = free.pop(0)
                    g = jobs[idx](sl)
                    if idx < NSL:
                        g = delayed(g, idx * STAG)
                    active.append((g, sl))
                    idx += 1
                for ent in list(active):
                    try:
                        next(ent[0])
                    except StopIteration:
                        active.remove(ent)
                        free.append(ent[1])

        for s in seqs:
            kb.dma(o_conv_p if s.i == 0 else o_conv_s[s.i - 1], s.proj[s.L:s.L + 3, :])
            kb.dma(gt[:], mods[1, s.i:s.i + 1, 2 * D:3 * D].partition_broadcast(128))
            for h_ in range(8):
                if s.i == 0:
                    kb.memset(S[h_][:], 0.0)
                else:
                    kb.dma(S[h_][:], gdn_in[s.i - 1, h_])
                kb.copy(Sb[h_][:], S[h_][:], act)
            tiles = list(range(0, s.L, 128))

            def tile_loads(ti):
                t0_ = tiles[ti]
                R_ = min(128, s.L - t0_)
                kb.dma(bat[:R_], s.ba[t0_:t0_ + R_, :])
                for j in range(4):
                    kb.dma(F[j][:R_], s.proj[t0_ + j:t0_ + j + R_, :])
                kb.dma(zts[ti % 3][:R_], s.z1[t0_:t0_ + R_, :])
                kb.dma(x1ts[ti % 3][:R_], s.x1[t0_:t0_ + R_, :])

            tile_loads(0)
            drain(prologue_gen(s, tiles[0], PBs[0], 0))
            for ti, t0 in enumerate(tiles):
                R = min(128, s.L - t0)
                chunks = [(0, min(64, R))] + ([(64, 128)] if R == 128 else [])
                pb = PBs[ti % 2]
                jobs = [(lambda sl, hv=hv: head_gen(hv, sl, R, chunks, pb)) for hv in range(8)]
                if ti >= 1:
                    jobs.append(lambda sl, ti=ti: epilogue_gen(s, tiles[ti - 1], PBs[(ti - 1) % 2], zts[(ti - 1) % 3],
                                                              x1ts[(ti - 1) % 3], sl))
                if ti + 1 < len(tiles):
                    tile_loads(ti + 1)
                    jobs.append(lambda sl, ti=ti: prologue_gen(s, tiles[ti + 1], PBs[(ti + 1) % 2], sl))
                run_tile(jobs)
            n_ = len(tiles)
            drain(epilogue_gen(s, tiles[n_ - 1], PBs[(n_ - 1) % 2], zts[(n_ - 1) % 3], x1ts[(n_ - 1) % 3], 0))
            og = o_gdn_p if s.i == 0 else o_gdn_s[s.i - 1]
            for h_ in range(8):
                kb.dma(og[h_], S[h_][:])
    kb.barrier()
    return kb


_CACHE = {}


def run(inputs, LP):
    if LP not in _CACHE:
        _CACHE[LP] = build(LP)
    kb = _CACHE[LP]
    f = lambda a: np.ascontiguousarray(np.asarray(a, dtype=np.float32))
    nb = inputs["x_prompt"].shape[0]
    in_maps = []
    for c in range(nb):
        m = {
            "xp": f(inputs["x_prompt"][c, :LP]),
            "xs": f(inputs["x_sample"][2 * c:2 * c + 2]),
            "cin": f(np.concatenate([inputs["c_prompt"][c:c + 1], inputs["c_sample"][2 * c:2 * c + 2]], 0)),
            "s5re_in": f(inputs["state_s5_re"][0, 2 * c:2 * c + 2]),
            "s5im_in": f(inputs["state_s5_im"][0, 2 * c:2 * c + 2]),
            "gdn_in": f(inputs["state_gdn"][0, 2 * c:2 * c + 2]),
            "conv_in": f(inputs["state_gdn_conv"][0, 2 * c:2 * c + 2]),
            "norm_g": f(inputs["norm_g"]), "w_ada": f(inputs["w_ada"]), "b_ada": f(inputs["b_ada"]),
            "s5_w_in": f(inputs["s5_w_in"][0]), "s5_log_step": f(inputs["s5_log_step"]),
            "s5_lre": f(inputs["s5_lambda_re"][0]), "s5_lim": f(inputs["s5_lambda_im"][0]),
            "s5_bre": f(inputs["s5_b_re"][0]), "s5_bim": f(inputs["s5_b_im"][0]),
            "s5_cre": f(inputs["s5_c_re"][0]), "s5_cim": f(inputs["s5_c_im"][0]),
            "s5_d": f(inputs["s5_d"]), "s5_w_glu": f(inputs["s5_w_glu"][0]), "s5_b_glu": f(inputs["s5_b_glu"]),
            "s5_w_out": f(inputs["s5_w_out"][0]), "gdn_w_in": f(inputs["gdn_w_in"][0]),
            "gdn_conv_w": f(inputs["gdn_conv_w"][0]), "gdn_a_log": f(inputs["gdn_a_log"]),
            "gdn_dt_bias": f(inputs["gdn_dt_bias"]), "gdn_norm_g": f(inputs["gdn_norm_g"]),
            "gdn_w_out": f(inputs["gdn_w_out"][0]), "final_g": f(inputs["final_g"].reshape(1, -1)),
        }
        in_maps.append(m)
    res = run_bass_kernel_spmd(kb.nc, in_maps, core_ids=list(range(nb)))
    r = res.results
    st = lambda k: np.stack([np.asarray(r[c][k], dtype=np.float32) for c in range(nb)], 0)
    cat = lambda k: np.concatenate([np.asarray(r[c][k], dtype=np.float32) for c in range(nb)], 0)
    return (st("yp"), cat("ys"), st("o_s5re_p")[None], st("o_s5im_p")[None], st("o_gdn_p")[None],
            st("o_conv_p")[None], cat("o_s5re_s")[None], cat("o_s5im_s")[None], cat("o_gdn_s")[None],
            cat("o_conv_s")[None])


def kernel(**inputs):
    return run(inputs, inputs["x_prompt"].shape[1])
```
